# Optimizing a Trainium2 kernel written in Bass

```python
import math
import jax, jax.numpy as jnp
from jax import lax
import numpy as np

D_MODEL = 2048
BATCH = 1
SEQ = 8192
DEPTH = 4

CHUNK = 64
HEAD_DIM = 128
A_Q_HEADS = 8
A_KV_HEADS = 2
A_WINDOW = 128
A_PREV_CHUNKS = A_WINDOW // CHUNK
B_HEADS = 4
B_QK_DIM = HEAD_DIM // 2
B_V_DIM = HEAD_DIM
B_QBLOCK = 128
C_HEADS = 4
C_PREV_CHUNKS = 8
REL_CLIP = 256
FFN_HIDDEN = int(math.ceil(8 * D_MODEL / 3 / 256)) * 256

ROPE_THETA = 10000.0
LN_EPS = 1e-5
DEEPNORM_ALPHA = (2 * DEPTH) ** 0.25
DEEPNORM_BETA = (8 * DEPTH) ** -0.25
NEG_INF = -1e30

A_Q_W = A_Q_HEADS * HEAD_DIM
A_KV_W = A_KV_HEADS * HEAD_DIM
B_QK_W = B_HEADS * 2 * B_QK_DIM
B_V_W = B_HEADS * B_V_DIM
C_W = C_HEADS * HEAD_DIM
IN_SPLITS = [A_Q_W, A_KV_W, A_KV_W, B_QK_W, B_QK_W, B_V_W, C_W, C_W, C_W]
V_SEGMENTS = (2, 5, 8)
IN_WIDTH = sum(IN_SPLITS)
IN_OFFSETS = [int(o) for o in np.cumsum(IN_SPLITS)[:-1]]

kernel_name = "hybrid_chunk_causal_gated_trunk"


def layer_norm(x, g, b):
    xf = x.astype(jnp.float32)
    mu = jnp.mean(xf, axis=-1, keepdims=True)
    var = jnp.mean(jnp.square(xf - mu), axis=-1, keepdims=True)
    y = (xf - mu) * lax.rsqrt(var + LN_EPS) * g.astype(jnp.float32) + b.astype(jnp.float32)
    return y.astype(x.dtype)


def rope_tables(seq, dim):
    pos = jnp.arange(seq, dtype=jnp.float32)
    inv = 1.0 / (ROPE_THETA ** (jnp.arange(0, dim, 2, dtype=jnp.float32) / dim))
    ang = pos[:, None] * inv[None, :]
    ang = jnp.concatenate([ang, ang], axis=-1)
    return jnp.cos(ang), jnp.sin(ang)


def apply_rope(t, cos, sin):
    half = t.shape[-1] // 2
    t1, t2 = t[..., :half], t[..., half:]
    rot = jnp.concatenate([-t2, t1], axis=-1)
    return t * cos.astype(t.dtype) + rot * sin.astype(t.dtype)


def to_heads(t, h):
    b, s, _ = t.shape
    return t.reshape(b, s, h, -1).transpose(0, 2, 1, 3)


def from_heads(t):
    b, h, s, d = t.shape
    return t.transpose(0, 2, 1, 3).reshape(b, s, h * d)


def band_gather(t, n_prev):
    b, h, s, d = t.shape
    nc = s // CHUNK
    tc = t.reshape(b, h, nc, CHUNK, d)
    tp = jnp.pad(tc, ((0, 0), (0, 0), (n_prev, 0), (0, 0), (0, 0)))
    band = jnp.stack([tp[:, :, i:i + nc] for i in range(n_prev + 1)], axis=3)
    return band.reshape(b, h, nc, (n_prev + 1) * CHUNK, d)


def band_valid(nc, n_prev):
    src = jnp.arange(nc)[:, None] + jnp.arange(n_prev + 1)[None, :] - n_prev
    return jnp.repeat(src >= 0, CHUNK, axis=1)


def window_sink_gqa(q, k, v, sinks):
    b, hq, s, d = q.shape
    hkv = k.shape[1]
    g = hq // hkv
    nc = s // CHUNK
    qc = q.reshape(b, hkv, g, nc, CHUNK, d)
    kb = band_gather(k, A_PREV_CHUNKS)
    vb = band_gather(v, A_PREV_CHUNKS)
    sc = jnp.einsum('bkgcqd,bkcjd->bkgcqj', qc, kb).astype(jnp.float32) * (d ** -0.5)
    valid = band_valid(nc, A_PREV_CHUNKS)
    sc = jnp.where(valid[:, None, :], sc, NEG_INF)
    sink = jnp.broadcast_to(sinks.astype(jnp.float32).reshape(1, hkv, g, 1, 1, 1), sc.shape[:-1] + (1,))
    p = jax.nn.softmax(jnp.concatenate([sc, sink], axis=-1), axis=-1)[..., :-1]
    o = jnp.einsum('bkgcqj,bkcjd->bkgcqd', p.astype(v.dtype), vb)
    return o.reshape(b, hq, s, d)


def diff_attention(q, k, v, lam, sub_g, lam_init):
    b, h, _, s, dq = q.shape
    dv = v.shape[-1]
    nb = s // B_QBLOCK
    qb = jnp.moveaxis(q.reshape(b, h, 2, nb, B_QBLOCK, dq), 3, 0)
    kchunk = jnp.arange(s) // CHUNK
    scale = dq ** -0.5

    def block(args):
        qi, bi = args
        qchunk = (bi * B_QBLOCK + jnp.arange(B_QBLOCK)) // CHUNK
        mask = kchunk[None, :] <= qchunk[:, None]
        sc = jnp.einsum('bhmqd,bhmkd->bhmqk', qi, k).astype(jnp.float32) * scale
        p = jax.nn.softmax(jnp.where(mask, sc, NEG_INF), axis=-1)
        a = p[:, :, 0] - lam * p[:, :, 1]
        return jnp.einsum('bhqk,bhkd->bhqd', a.astype(v.dtype), v)

    o = lax.map(block, (qb, jnp.arange(nb)))
    o = jnp.moveaxis(o, 0, 2).reshape(b, h, s, dv)
    of = o.astype(jnp.float32)
    of = of * lax.rsqrt(jnp.mean(jnp.square(of), axis=-1, keepdims=True) + LN_EPS)
    of = of * sub_g.astype(jnp.float32) * (1.0 - lam_init)
    return of.astype(v.dtype)


def chunk_relbias_attention(q, k, v, rel_bias):
    b, h, s, d = q.shape
    nc = s // CHUNK
    j = (C_PREV_CHUNKS + 1) * CHUNK
    qc = q.reshape(b, h, nc, CHUNK, d)
    kb = band_gather(k, C_PREV_CHUNKS)
    vb = band_gather(v, C_PREV_CHUNKS)
    dist = C_PREV_CHUNKS * CHUNK + jnp.arange(CHUNK)[:, None] - jnp.arange(j)[None, :]
    idx = jnp.clip(dist, -REL_CLIP, REL_CLIP) + REL_CLIP
    bias = rel_bias.astype(jnp.float32)[:, idx]
    sc = jnp.einsum('bhcqd,bhcjd->bhcqj', qc, kb).astype(jnp.float32) * (d ** -0.5) + bias[None, :, None]
    valid = band_valid(nc, C_PREV_CHUNKS)
    sc = jnp.where(valid[:, None, :], sc, NEG_INF)
    p = jax.nn.softmax(sc, axis=-1)
    o = jnp.einsum('bhcqj,bhcjd->bhcqd', p.astype(v.dtype), vb)
    return o.reshape(b, h, s, d)


def setup_inputs(seed: int = 0) -> dict:
    key = jax.random.key(seed)
    ks = jax.random.split(key, 24)
    f32 = jnp.float32
    col_scale = jnp.concatenate([
        jnp.full((w,), DEEPNORM_BETA if i in V_SEGMENTS else 1.0, f32) for i, w in enumerate(IN_SPLITS)])
    nrm = lambda k, shape: jax.random.normal(k, shape, f32)
    return {
        "x": nrm(ks[0], (BATCH, SEQ, D_MODEL)),
        "w_in": nrm(ks[1], (DEPTH, D_MODEL, IN_WIDTH)) * (D_MODEL ** -0.5) * col_scale,
        "sinks": nrm(ks[2], (DEPTH, A_Q_HEADS)) * 0.5,
        "lambda_q1": nrm(ks[3], (DEPTH, B_QK_DIM)) * 0.1,
        "lambda_k1": nrm(ks[4], (DEPTH, B_QK_DIM)) * 0.1,
        "lambda_q2": nrm(ks[5], (DEPTH, B_QK_DIM)) * 0.1,
        "lambda_k2": nrm(ks[6], (DEPTH, B_QK_DIM)) * 0.1,
        "diff_norm_g": 1.0 + 0.02 * nrm(ks[7], (DEPTH, B_V_DIM)),
        "rel_bias": nrm(ks[8], (DEPTH, C_HEADS, 2 * REL_CLIP + 1)) * 0.1,
        "w_br_a": nrm(ks[9], (DEPTH, A_Q_W, D_MODEL)) * (A_Q_W ** -0.5),
        "w_br_b": nrm(ks[10], (DEPTH, B_V_W, D_MODEL)) * (B_V_W ** -0.5),
        "w_br_c": nrm(ks[11], (DEPTH, C_W, D_MODEL)) * (C_W ** -0.5),
        "w_gate": nrm(ks[12], (DEPTH, D_MODEL, 3 * D_MODEL)) * (D_MODEL ** -0.5),
        "b_gate": nrm(ks[13], (DEPTH, 3 * D_MODEL)) * 0.02,
        "w_out": nrm(ks[14], (DEPTH, D_MODEL, D_MODEL)) * (D_MODEL ** -0.5) * DEEPNORM_BETA,
        "ln1_g": 1.0 + 0.02 * nrm(ks[15], (DEPTH, D_MODEL)),
        "ln1_b": 0.02 * nrm(ks[16], (DEPTH, D_MODEL)),
        "w_ffn_in": nrm(ks[17], (DEPTH, D_MODEL, 2 * FFN_HIDDEN)) * (D_MODEL ** -0.5),
        "w_ffn_out": nrm(ks[18], (DEPTH, FFN_HIDDEN, D_MODEL)) * (FFN_HIDDEN ** -0.5) * DEEPNORM_BETA,
        "ln2_g": 1.0 + 0.02 * nrm(ks[19], (DEPTH, D_MODEL)),
        "ln2_b": 0.02 * nrm(ks[20], (DEPTH, D_MODEL)),
    }


def reference(x, w_in, sinks, lambda_q1, lambda_k1, lambda_q2, lambda_k2, diff_norm_g, rel_bias,
              w_br_a, w_br_b, w_br_c, w_gate, b_gate, w_out, ln1_g, ln1_b,
              w_ffn_in, w_ffn_out, ln2_g, ln2_b):
    b, s, _ = x.shape
    cos_a, sin_a = rope_tables(s, HEAD_DIM)
    cos_b, sin_b = rope_tables(s, B_QK_DIM)
    for l in range(DEPTH):
        h = x @ w_in[l]
        aq, ak, av, bq, bk, bv, cq, ck, cv = jnp.split(h, IN_OFFSETS, axis=-1)
        aq = apply_rope(to_heads(aq, A_Q_HEADS), cos_a, sin_a)
        ak = apply_rope(to_heads(ak, A_KV_HEADS), cos_a, sin_a)
        ya = from_heads(window_sink_gqa(aq, ak, to_heads(av, A_KV_HEADS), sinks[l]))
        bq = apply_rope(bq.reshape(b, s, B_HEADS, 2, B_QK_DIM).transpose(0, 2, 3, 1, 4), cos_b, sin_b)
        bk = apply_rope(bk.reshape(b, s, B_HEADS, 2, B_QK_DIM).transpose(0, 2, 3, 1, 4), cos_b, sin_b)
        lam_init = 0.8 - 0.6 * math.exp(-0.3 * l)
        lam = (jnp.exp(jnp.sum(lambda_q1[l].astype(jnp.float32) * lambda_k1[l].astype(jnp.float32)))
               - jnp.exp(jnp.sum(lambda_q2[l].astype(jnp.float32) * lambda_k2[l].astype(jnp.float32)))
               + lam_init)
        yb = from_heads(diff_attention(bq, bk, to_heads(bv, B_HEADS), lam, diff_norm_g[l], lam_init))
        yc = from_heads(chunk_relbias_attention(to_heads(cq, C_HEADS), to_heads(ck, C_HEADS),
                                                to_heads(cv, C_HEADS), rel_bias[l]))
        ga, gb, gc = jnp.split(jax.nn.sigmoid(x @ w_gate[l] + b_gate[l]), 3, axis=-1)
        mix = ga * (ya @ w_br_a[l]) + gb * (yb @ w_br_b[l]) + gc * (yc @ w_br_c[l])
        x = layer_norm(DEEPNORM_ALPHA * x + mix @ w_out[l], ln1_g[l], ln1_b[l])
        f_gate, f_up = jnp.split(x @ w_ffn_in[l], 2, axis=-1)
        x = layer_norm(DEEPNORM_ALPHA * x + (jax.nn.silu(f_gate) * f_up) @ w_ffn_out[l], ln2_g[l], ln2_b[l])
    return x
```

```python
import math
from contextlib import ExitStack

import numpy as np
import ml_dtypes

import concourse.bass as bass
import concourse.mybir as mybir
from concourse.bass_utils import run_bass_kernel_spmd

F32 = mybir.dt.float32
BF16 = mybir.dt.bfloat16
AF = mybir.ActivationFunctionType
ALU = mybir.AluOpType
AX = mybir.AxisListType

NCORES = 8
D = 2048
S = 8192
TPC = S // NCORES
DEPTH = 4
FFN = 5632
NEG = -30000.0
ALPHA = (2 * DEPTH) ** 0.25
EPS = 1e-5

ENGS = ("pe", "act", "dve", "pool", "sp")


class Prog:
    def __init__(self, nc):
        self.nc = nc
        self.ops = {e: [] for e in ENGS}
        self.semcnt = {}

    def add(self, eng, fn, waits=(), sig=None, inc=1):
        tok = None
        if sig is not None:
            self.semcnt[sig] = self.semcnt.get(sig, 0) + inc
            tok = (sig, self.semcnt[sig])
        ws = []
        for w in waits:
            if w is None:
                continue
            if isinstance(w, list):
                ws.extend([x for x in w if x is not None])
            else:
                ws.append(w)
        self.ops[eng].append((fn, ws, sig, inc))
        return tok

    def mm(self, out, lhsT, rhs, start, stop, waits=(), sig=False):
        return self.add("pe", lambda e: e.matmul(out, lhsT, rhs, start=start, stop=stop),
                        waits, "pe" if sig else None)

    def act(self, out, in_, func, bias=None, scale=1.0, waits=(), eng="act"):
        if bias is None:
            fn = lambda e: e.activation(out=out, in_=in_, func=func, scale=scale)
        else:
            fn = lambda e: e.activation(out=out, in_=in_, func=func, bias=bias, scale=scale)
        return self.add("act", fn, waits, "act")

    def tt(self, eng, out, in0, in1, op, waits=()):
        return self.add(eng, lambda e: e.tensor_tensor(out=out, in0=in0, in1=in1, op=op), waits, eng)

    def ts(self, eng, out, in0, s1, op0, s2=None, op1=None, waits=()):
        if op1 is None:
            fn = lambda e: e.tensor_scalar(out=out, in0=in0, scalar1=s1, scalar2=None, op0=op0)
        else:
            fn = lambda e: e.tensor_scalar(out=out, in0=in0, scalar1=s1, scalar2=s2, op0=op0, op1=op1)
        return self.add(eng, fn, waits, eng)

    def stt(self, eng, out, in0, scalar, in1, op0, op1, waits=()):
        return self.add(eng, lambda e: e.scalar_tensor_tensor(out=out, in0=in0, scalar=scalar, in1=in1,
                                                              op0=op0, op1=op1), waits, eng)

    def copy(self, eng, out, in_, waits=()):
        if eng == "act":
            return self.add("act", lambda e: e.copy(out=out, in_=in_), waits, "act")
        return self.add(eng, lambda e: e.tensor_copy(out=out, in_=in_), waits, eng)

    def rsum(self, eng, out, in_, waits=()):
        return self.add(eng, lambda e: e.reduce_sum(out=out, in_=in_, axis=AX.X), waits, eng)

    def recip(self, out, in_, waits=()):
        return self.add("dve", lambda e: e.reciprocal(out=out, in_=in_), waits, "dve")

    def dma(self, q, out, in_, sem, waits=()):
        return self.add(q, lambda e: e.dma_start(out=out, in_=in_), waits, sem, 16)

    def emit(self, final_waits):
        nc = self.nc
        self.add("sp", lambda e: e.nop(), final_waits, None)
        with ExitStack() as st:
            sems = {k: st.enter_context(nc.semaphore("s_" + k)) for k in self.semcnt}
            block = st.enter_context(nc.Block())

            def run(e, name):
                waited = {}
                for fn, ws, sig, inc in self.ops[name]:
                    for (k, v) in ws:
                        if waited.get(k, 0) < v:
                            e.wait_ge(sems[k], v)
                            waited[k] = v
                    ins = fn(e)
                    if sig is not None:
                        ins.then_inc(sems[sig], inc)

            @block.tensor
            def _(e):
                run(e, "pe")

            @block.scalar
            def _(e):
                run(e, "act")

            @block.vector
            def _(e):
                run(e, "dve")

            @block.gpsimd
            def _(e):
                run(e, "pool")

            @block.sync
            def _(e):
                run(e, "sp")


class Arena:
    def __init__(self, t):
        self.t = t
        self.off = 0
        self.cap = t.shape[1]

    def mark(self):
        return self.off

    def reset(self, m):
        self.off = m

    def alloc(self, shape_free, dtype):
        n = 1
        for s in shape_free:
            n *= s
        nel = n * (2 if dtype == F32 else 1)
        nel = (nel + 31) // 32 * 32
        assert self.off + nel <= self.cap, ("SBUF arena overflow", self.off, nel, self.cap)
        v = self.t[:, self.off:self.off + nel]
        self.off += nel
        if dtype == F32:
            v = v.bitcast(F32)
        v = v[:, 0:n]
        if len(shape_free) == 2:
            v = v.rearrange("p (a b) -> p a b", a=shape_free[0])
        elif len(shape_free) == 3:
            v = v.rearrange("p (a b c) -> p a b c", a=shape_free[0], b=shape_free[1])
        return v


class Ring:
    def __init__(self, bufs):
        self.bufs = bufs
        self.free = [[] for _ in bufs]
        self.i = 0

    def next(self):
        k = self.i % len(self.bufs)
        self.i += 1
        fr = self.free[k]
        self.free[k] = []
        return k, self.bufs[k], fr

    def release(self, k, *toks):
        self.free[k].extend([t for t in toks if t is not None])


def new_nc():
    return bass.Bass("TRN2", target_bir_lowering=False)


def build_in():
    nc = new_nc()
    xT = nc.dram_tensor("xT", [D, TPC], F32, kind="ExternalInput").ap()
    w = nc.dram_tensor("w", [D, 4608], F32, kind="ExternalInput").ap()
    tabs = nc.dram_tensor("tabs", [4, 128, TPC], F32, kind="ExternalInput").ap()
    rmat = nc.dram_tensor("rmat", [2, 128, 128], F32, kind="ExternalInput").ap()
    qT = nc.dram_tensor("qT", [2048, TPC], BF16, kind="ExternalOutput").ap()
    kT = nc.dram_tensor("kT", [1280, TPC], BF16, kind="ExternalOutput").ap()
    vo = nc.dram_tensor("v", [TPC, 1280], BF16, kind="ExternalOutput").ap()
    P = Prog(nc)
    with ExitStack() as st:
        st.enter_context(nc.allow_low_precision("bf16 matmul operands, fp32 accumulation"))
        arena_t = st.enter_context(nc.sbuf_tensor("arena", [128, 100000], BF16))
        A = Arena(arena_t)
        ps = [st.enter_context(nc.psum_tensor("ps%d" % i, [128, 512], F32))[:, :] for i in range(8)]
        xTb = A.alloc([16, TPC], BF16)
        wb = [A.alloc([16, 512], BF16) for _ in range(2)]
        tab = A.alloc([4, TPC], F32)
        rm = A.alloc([2, 128], BF16)
        tbf = [A.alloc([512], BF16) for _ in range(2)]
        tma = [A.alloc([512], F32) for _ in range(2)]
        tmb = [A.alloc([512], F32) for _ in range(2)]
        ost = [A.alloc([TPC], BF16) for _ in range(2)]
        vst = [A.alloc([512], BF16) for _ in range(2)]

        xTv = xT.rearrange("(k p) t -> p k t", p=128)
        for g in range(4):
            t_x = P.dma("pool", xTb[:, g * 4:(g + 1) * 4, :], xTv[:, g * 4:(g + 1) * 4, :], "ld_x")
        t_tab = P.dma("sp", tab, tabs.rearrange("a p t -> p a t"), "ld_tab")
        t_rm = P.dma("pool", rm, rmat.rearrange("a p n -> p a n"), "ld_rm")

        wring = Ring(wb)
        acc = Ring([ps[0], ps[1], ps[2], ps[3]])
        rot = Ring([ps[4], ps[5]])
        tbf_r = Ring(tbf)
        tma_r = Ring(tma)
        tmb_r = Ring(tmb)
        ost_r = Ring(ost)
        vst_r = Ring(vst)
        out_toks = []
        wload_n = [0, 0]

        blocks = {
            0: [("ropeA", c, qT, c * 128) for c in range(4)],
            1: [("ropeA", c, qT, 512 + c * 128) for c in range(4)],
            2: [("ropeA", 0, kT, 0), ("ropeA", 1, kT, 128), ("v", 256, 256, 0)],
            3: [("ropeB", c, qT, 1024 + c * 128) for c in range(4)],
            4: [("ropeB", c, kT, 256 + c * 128) for c in range(4)],
            5: [("v", 0, 512, 256)],
            6: [("plain", c, qT, 1536 + c * 128) for c in range(4)],
            7: [("plain", c, kT, 768 + c * 128) for c in range(4)],
            8: [("v", 0, 512, 768)],
        }
        pending = []

        def flush_pending():
            while pending:
                pending.pop(0)()

        import os
        blist = [int(t) for t in os.environ.get('KBLOCKS', '0,1,2,3,4,5,6,7,8').split(',')]
        def load_w(b):
            ws, wbuf, wfree = wring.next()
            wv = w[:, b * 512:(b + 1) * 512].rearrange("(k p) n -> p k n", p=128)
            for g in range(2):
                t_w = P.dma("pool", wbuf[:, g * 8:(g + 1) * 8, :], wv[:, g * 8:(g + 1) * 8, :],
                            "ld_w%d" % ws, waits=wfree)
            return ws, wbuf, t_w

        nxt = load_w(blist[0])
        for bi, b in enumerate(blist):
            ws, wbuf, t_w = nxt
            if bi + 1 < len(blist):
                nxt = load_w(blist[bi + 1])
            last_mm = None
            for item in blocks[b]:
                if item[0] == "v":
                    _, c0, width, vcol = item
                    for tt in range(8):
                        ak, aps, afree = acc.next()
                        for kc in range(16):
                            last_mm = P.mm(aps[:, 0:width], xTb[:, kc, tt * 128:(tt + 1) * 128],
                                           wbuf[:, kc, c0:c0 + width], kc == 0, kc == 15,
                                           waits=[t_x, t_w] + afree if kc == 0 else (), sig=(kc == 15))
                        flush_pending()
                        vk, vb, vfree = vst_r.next()
                        t_c = P.copy("act", vb[:, 0:width], aps[:, 0:width], waits=[last_mm] + vfree)
                        acc.release(ak, t_c)
                        t_o = P.dma("sp", vo[tt * 128:(tt + 1) * 128, vcol:vcol + width], vb[:, 0:width],
                                    "st_v%d" % vk, waits=[t_c])
                        vst_r.release(vk, t_o)
                        out_toks.append(t_o)
                else:
                    kind, c, dst, row0 = item
                    ok, ob, ofree = ost_r.next()
                    done = []
                    for tb in range(2):
                        ak, aps, afree = acc.next()
                        for kc in range(16):
                            last_mm = P.mm(aps, wbuf[:, kc, c * 128:(c + 1) * 128],
                                           xTb[:, kc, tb * 512:(tb + 1) * 512], kc == 0, kc == 15,
                                           waits=[t_x, t_w] + afree if kc == 0 else (), sig=(kc == 15))
                        flush_pending()
                        osl = ob[:, tb * 512:(tb + 1) * 512]
                        if kind == "plain":
                            t_c = P.copy("act", osl, aps, waits=[last_mm] + (ofree if tb == 0 else []))
                            acc.release(ak, t_c)
                            done.append(t_c)
                        else:
                            ti = 0 if kind == "ropeA" else 2
                            ri = 0 if kind == "ropeA" else 1
                            bk_, bb, bfree = tbf_r.next()
                            t_c = P.copy("act", bb, aps, waits=[last_mm] + bfree)
                            mk, ma, mfree = tma_r.next()
                            t_a = P.tt("dve", ma, aps, tab[:, ti, tb * 512:(tb + 1) * 512], ALU.mult,
                                       waits=[last_mm, t_tab, t_c] + mfree)
                            acc.release(ak, t_c, t_a)

                            def deferred(bb=bb, bk_=bk_, t_c=t_c, ri=ri, ti=ti, tb=tb, osl=osl, ma=ma, mk=mk,
                                         t_a=t_a, done=done, ofree=(ofree if tb == 0 else [])):
                                rk, rps, rfree = rot.next()
                                t_r = P.mm(rps, rm[:, ri, :], bb, True, True, waits=[t_c, t_rm] + rfree, sig=True)
                                tbf_r.release(bk_, t_r)
                                nk, mb, nfree = tmb_r.next()
                                t_b = P.tt("dve", mb, rps, tab[:, ti + 1, tb * 512:(tb + 1) * 512], ALU.mult,
                                           waits=[t_r] + nfree)
                                rot.release(rk, t_b)
                                t_s = P.tt("dve", osl, ma, mb, ALU.add, waits=[t_a, t_b] + ofree)
                                tma_r.release(mk, t_s)
                                tmb_r.release(nk, t_s)
                                done.append(t_s)
                            pending.append(deferred)
                    if kind != "plain":
                        flush_pending()
                    t_o = P.dma("sp", dst[row0:row0 + 128, :], ob, "st_o%d" % ok, waits=done)
                    ost_r.release(ok, t_o)
                    out_toks.append(t_o)
            wring.release(ws, last_mm)
        flush_pending()
        P.emit(out_toks)
    return nc


def rope_tabs():
    pos = np.arange(S, dtype=np.float32)

    def tab(dim):
        inv = (1.0 / (np.float32(10000.0) ** (np.arange(0, dim, 2, dtype=np.float32) / np.float32(dim)))).astype(np.float32)
        ang = (pos[:, None] * inv[None, :]).astype(np.float32)
        ang = np.concatenate([ang, ang], axis=-1)
        return np.cos(ang).astype(np.float32).T, np.sin(ang).astype(np.float32).T

    cA, sA = tab(128)
    cB, sB = tab(64)
    cB = np.concatenate([cB, cB], 0)
    sB = np.concatenate([sB, sB], 0)
    full = np.stack([cA, sA, cB, sB], 0)
    return [np.ascontiguousarray(full[:, :, c * TPC:(c + 1) * TPC]) for c in range(NCORES)]


def rope_rmats():
    RA = np.zeros((128, 128), np.float32)
    for n2 in range(128):
        if n2 < 64:
            RA[n2 + 64, n2] = -1.0
        else:
            RA[n2 - 64, n2] = 1.0
    RB = np.zeros((128, 128), np.float32)
    for cb in (0, 64):
        for j in range(64):
            if j < 32:
                RB[cb + j + 32, cb + j] = -1.0
            else:
                RB[cb + j - 32, cb + j] = 1.0
    return np.stack([RA, RB], 0)


_NC_CACHE = {}


def get_prog(name, builder):
    if name not in _NC_CACHE:
        _NC_CACHE[name] = builder()
    return _NC_CACHE[name]


def launch(name, builder, in_maps):
    nc = get_prog(name, builder)
    res = run_bass_kernel_spmd(nc, in_maps, core_ids=list(range(NCORES)))
    return res.results


def run_in(xT_list, w_in_l):
    tabs = rope_tabs()
    rm = rope_rmats()
    in_maps = [{"xT": xT_list[c], "w": w_in_l, "tabs": tabs[c], "rmat": rm} for c in range(NCORES)]
    return launch("in", build_in, in_maps)


def build_attn():
    nc = new_nc()
    din = lambda n, s, d: nc.dram_tensor(n, s, d, kind="ExternalInput").ap()
    qT = din("qT", [2048, TPC], BF16)
    kA = din("kA", [256, 1152], BF16)
    vA = din("vA", [1152, 256], BF16)
    kB = din("kB", [512, S], BF16)
    vB = din("vB", [S, 512], BF16)
    kC = din("kC", [512, 1536], BF16)
    vC = din("vC", [1536, 512], BF16)
    bmA_d = din("bmA", [6, 128, 512], F32)
    bmB_d = din("bmB", [4, 128, 512], F32)
    bmC_d = din("bmC", [4, 12, 128, 512], F32)
    colB_d = din("colB", [128, 56], F32)
    sinks_d = din("sinks", [1, 8], F32)
    lamv_d = din("lamv", [1, 256], F32)
    gsub_d = din("gsub", [1, 128], F32)
    cst_d = din("cst", [1, 2], F32)
    yT = nc.dram_tensor("yT", [2048, TPC], BF16, kind="ExternalOutput").ap()
    P = Prog(nc)
    SCA = 128.0 ** -0.5
    SCB = 64.0 ** -0.5
    with ExitStack() as st:
        st.enter_context(nc.allow_low_precision("bf16 matmul operands, fp32 accumulation"))
        arena_t = st.enter_context(nc.sbuf_tensor("arena", [128, 102000], BF16))
        A = Arena(arena_t)
        ps = [st.enter_context(nc.psum_tensor("ps%d" % i, [128, 512], F32))[:, :] for i in range(8)]
        ones = A.alloc([128], BF16)
        expsink = A.alloc([8], F32)
        lamv = A.alloc([4, 64], F32)
        lamp = A.alloc([2, 64], F32)
        lams = A.alloc([4], F32)
        cst = A.alloc([2], F32)
        gsc = A.alloc([2], F32)
        colB = A.alloc([56], F32)
        qh = [A.alloc([TPC], BF16) for _ in range(2)]
        Eb = [A.alloc([512], BF16) for _ in range(6)]
        tmpb = [A.alloc([512], F32) for _ in range(3)]
        yst = [A.alloc([TPC], BF16) for _ in range(2)]
        fin = [A.alloc([512], F32) for _ in range(4)]
        sqb = A.alloc([512], BF16)
        mark = A.mark()

        t_c0 = P.add("dve", lambda e: e.memset(ones, 1.0), (), "dve")
        t_l = P.dma("sp", expsink, sinks_d.partition_broadcast(128), "ld_misc")
        t_l = P.dma("sp", lamv.rearrange("p a b -> p (a b)"), lamv_d.partition_broadcast(128), "ld_misc")
        t_l = P.dma("sp", cst, cst_d.partition_broadcast(128), "ld_misc")
        t_l = P.dma("sp", gsc[:, 0:1], gsub_d.rearrange("a d -> d a"), "ld_misc")
        t_l = P.dma("sp", colB, colB_d, "ld_misc")
        t_es = P.act(expsink, expsink, AF.Exp, waits=[t_l])
        t1 = P.tt("dve", lamp[:, 0, :], lamv[:, 0, :], lamv[:, 1, :], ALU.mult, waits=[t_l])
        t1 = P.tt("dve", lamp[:, 1, :], lamv[:, 2, :], lamv[:, 3, :], ALU.mult, waits=[t1])
        t1 = P.rsum("dve", lams[:, 0:1], lamp[:, 0, :], waits=[t1])
        t1 = P.rsum("dve", lams[:, 1:2], lamp[:, 1, :], waits=[t1])
        t2 = P.act(lams[:, 0:2], lams[:, 0:2], AF.Exp, waits=[t1])
        t1 = P.tt("dve", lams[:, 2:3], lams[:, 0:1], lams[:, 1:2], ALU.subtract, waits=[t2])
        t1 = P.tt("dve", lams[:, 2:3], lams[:, 2:3], cst[:, 0:1], ALU.add, waits=[t1])
        t1 = P.ts("dve", lams[:, 3:4], lams[:, 2:3], -1.0, ALU.mult, waits=[t1])
        t_prep = P.tt("dve", gsc[:, 1:2], gsc[:, 0:1], cst[:, 1:2], ALU.mult, waits=[t1])

        STr = Ring([ps[0], ps[1], ps[2], ps[3]])
        Er = Ring(Eb)
        Tr = Ring(tmpb)
        qr = Ring(qh)
        yr = Ring(yst)
        pending = []
        DEPTH = 2
        out_toks = []
        lastpe = [None]

        def flush(keep):
            while len(pending) > keep:
                pending.pop(0)()

        def load_q(row0):
            k, buf, fr = qr.next()
            t = P.dma("sp", buf, qT[row0:row0 + 128, :], "ld_q%d" % k, waits=fr)
            return k, buf, t

        kAs = A.alloc([2, 1152], BF16)
        vAs = A.alloc([9, 256], BF16)
        kCs = A.alloc([4, 1536], BF16)
        vCs = A.alloc([12, 512], BF16)
        bmA = A.alloc([6, 512], F32)
        bmC = A.alloc([12, 512], F32)
        t_ka = P.dma("sp", kAs, kA.rearrange("(h p) t -> p h t", p=128), "ld_ka")
        t_va = P.dma("sp", vAs, vA.rearrange("(w p) d -> p w d", p=128), "ld_va")
        t_bma = P.dma("sp", bmA, bmA_d.rearrange("a p q -> p a q"), "ld_bma")
        t_kc = P.dma("sp", kCs, kC.rearrange("(h p) t -> p h t", p=128), "ld_kc")
        t_vc = P.dma("sp", vCs, vC.rearrange("(w p) d -> p w d", p=128), "ld_vc")
        bmc_free = []
        unit_n = [0]

        def single_unit(qbuf, t_q, j, steps, scale, fin_fn):
            n = unit_n[0]
            unit_n[0] += 1
            O_ps = ps[4 + n % 2]
            D_ps = ps[6 + n % 2]
            ofree = single_unit.free[n % 2]
            single_unit.free[n % 2] = []
            state = {"last": None}
            ns = len(steps)
            for idx, (k_ap, v_ap, bm_ap, ltoks) in enumerate(steps):
                sk, sps, sfree = STr.next()
                t_s = P.mm(sps, k_ap, qbuf[:, j * 512:(j + 1) * 512], True, True,
                           waits=[t_q] + ltoks + sfree, sig=True)
                mk, tmp, tfree = Tr.next()
                t_d = P.stt("dve", tmp, sps, scale, bm_ap, ALU.mult, ALU.add, waits=[t_s] + ltoks + tfree)
                STr.release(sk, t_d)
                ek, E, efree = Er.next()
                t_e = P.act(E, tmp, AF.Exp, waits=[t_d] + efree)
                Tr.release(mk, t_e)

                def pv(idx=idx, v_ap=v_ap, E=E, ek=ek, t_e=t_e, ltoks=ltoks):
                    w0 = [t_e] + ltoks + (ofree if idx == 0 else [])
                    P.mm(O_ps, v_ap, E, idx == 0, idx == ns - 1, waits=w0)
                    t = P.mm(D_ps, ones, E, idx == 0, idx == ns - 1, waits=[t_c0], sig=True)
                    Er.release(ek, t)
                    state["last"] = t
                    lastpe[0] = t
                pending.append(pv)
                flush(DEPTH - 1)

            def finalize():
                toks = fin_fn(O_ps, D_ps, state["last"])
                single_unit.free[n % 2] = toks
            pending.append(finalize)
        single_unit.free = [[], []]

        fr_ = Ring(fin)

        def make_fin(kind, hq, j, ybuf, yfree, done):
            def f(O_ps, D_ps, t_last):
                fk, d, ffree = fr_.next()
                if kind == "A":
                    t = P.ts("dve", d, D_ps, expsink[:, hq:hq + 1], ALU.add, waits=[t_last, t_es] + ffree)
                    t = P.recip(d, d, waits=[t])
                else:
                    t = P.recip(d, D_ps, waits=[t_last] + ffree)
                t = P.tt("dve", ybuf[:, j * 512:(j + 1) * 512], O_ps, d, ALU.mult, waits=[t] + yfree)
                fr_.release(fk, t)
                done.append(t)
                return [t]
            return f

        for hq in range(8):
            qk, qbuf, t_q = load_q(hq * 128)
            yk, ybuf, yfree = yr.next()
            done = []
            kvh = hq // 4
            for j in range(2):
                steps = []
                for i in range(5):
                    w_ = 4 * j + i
                    e_ = 5 if (j == 0 and i == 0) else i
                    steps.append((kAs[:, kvh, w_ * 128:(w_ + 1) * 128], vAs[:, w_, kvh * 128:(kvh + 1) * 128],
                                  bmA[:, e_, :], [t_ka, t_va, t_bma]))
                single_unit(qbuf, t_q, j, steps, SCA, make_fin("A", hq, j, ybuf, yfree if j == 0 else [], done))
            flush(0)
            qr.release(qk, lastpe[0])
            t_o = P.dma("pool", yT[hq * 128:(hq + 1) * 128, :], ybuf, "st_y%d" % yk, waits=done)
            yr.release(yk, t_o)
            out_toks.append(t_o)

        for h in range(4):
            qk, qbuf, t_q = load_q(1536 + h * 128)
            t_bmc = P.dma("sp", bmC, bmC_d[h].rearrange("a p q -> p a q"), "ld_bmc", waits=bmc_free)
            yk, ybuf, yfree = yr.next()
            done = []
            for j in range(2):
                steps = []
                for i in range(8):
                    w_ = 4 * j + i
                    e_ = (8 + i) if (j == 0 and i < 4) else i
                    steps.append((kCs[:, h, w_ * 128:(w_ + 1) * 128], vCs[:, w_, h * 128:(h + 1) * 128],
                                  bmC[:, e_, :], [t_kc, t_vc, t_bmc]))
                single_unit(qbuf, t_q, j, steps, SCA, make_fin("C", h, j, ybuf, yfree if j == 0 else [], done))
            flush(0)
            qr.release(qk, lastpe[0])
            bmc_free = [("dve", P.semcnt["dve"])]
            t_o = P.dma("pool", yT[1536 + h * 128:1536 + (h + 1) * 128, :], ybuf, "st_y%d" % yk, waits=done)
            yr.release(yk, t_o)
            out_toks.append(t_o)

        phase_done = [("pe", P.semcnt["pe"]), ("dve", P.semcnt["dve"]), ("act", P.semcnt["act"])]
        A.reset(mark)
        vBs = A.alloc([64, 512], BF16)
        kBs = [A.alloc([S], BF16) for _ in range(2)]
        bmB = A.alloc([4, 512], F32)
        t_bmb = P.dma("sp", bmB, bmB_d.rearrange("a p q -> p a q"), "ld_bmb", waits=phase_done)
        vBv = vB.rearrange("(t p) d -> p t d", p=128)
        for g in range(8):
            t_vb = P.dma("sp", vBs[:, g * 8:(g + 1) * 8, :], vBv[:, g * 8:(g + 1) * 8, :], "ld_vb", waits=phase_done)
        kbr = Ring(kBs)
        bfree = [[], [], [], []]

        for h in range(4):
            qk, qbuf, t_q = load_q(1024 + h * 128)
            kk, kbuf, kfree = kbr.next()
            for g in range(2):
                t_kb = P.dma("sp", kbuf[:, g * 4096:(g + 1) * 4096], kB[h * 128:(h + 1) * 128, g * 4096:(g + 1) * 4096],
                             "ld_kb%d" % kk, waits=kfree + phase_done)
            yk, ybuf, yfree = yr.next()
            done = []
            for j in range(2):
                if j == 0:
                    tiles = [(t, "diag", t) for t in range(4)]
                else:
                    tiles = [(t, "full", 0) for t in range(4)] + [(t, "diag", t - 4) for t in range(4, 8)]
                tiles += [(t, "other", t - 8) for t in range(8, 64)]
                nt = len(tiles)
                state = {"last": [None, None]}
                for idx, (t, kind, arg) in enumerate(tiles):
                    Es = []
                    for c in range(2):
                        sk, sps, sfree = STr.next()
                        t_s = P.mm(sps, kbuf[c * 64:(c + 1) * 64, t * 128:(t + 1) * 128],
                                   qbuf[c * 64:(c + 1) * 64, j * 512:(j + 1) * 512], True, True,
                                   waits=[t_q, t_kb] + sfree, sig=True)
                        ek, E, efree = Er.next()
                        if kind == "diag":
                            mk, tmp, tfree = Tr.next()
                            t_d = P.stt("dve", tmp, sps, SCB, bmB[:, arg, :], ALU.mult, ALU.add,
                                        waits=[t_s, t_bmb] + tfree)
                            STr.release(sk, t_d)
                            t_e = P.act(E, tmp, AF.Exp, waits=[t_d] + efree)
                            Tr.release(mk, t_e)
                        elif kind == "full":
                            t_e = P.act(E, sps, AF.Exp, scale=SCB, waits=[t_s] + efree)
                            STr.release(sk, t_e)
                        else:
                            t_e = P.act(E, sps, AF.Exp, bias=colB[:, arg:arg + 1], scale=SCB,
                                        waits=[t_s, t_l] + efree)
                            STr.release(sk, t_e)
                        Es.append((ek, E, t_e))

                    def pv(idx=idx, t=t, Es=Es, nt=nt, state=state):
                        for c in range(2):
                            ek, E, t_e = Es[c]
                            w0 = [t_e, t_vb] + (bfree[c] + bfree[2 + c] if idx == 0 else [])
                            P.mm(ps[4 + c], vBs[:, t, h * 128:(h + 1) * 128], E, idx == 0, idx == nt - 1, waits=w0)
                            tk = P.mm(ps[6 + c], ones, E, idx == 0, idx == nt - 1, waits=[t_c0], sig=True)
                            Er.release(ek, tk)
                            state["last"][c] = tk
                            lastpe[0] = tk
                    pending.append(pv)
                    flush(DEPTH - 1)
                flush(0)
                tl0, tl1 = state["last"]
                f0, f1, f2, f3 = fin
                wfin = [("dve", P.semcnt["dve"])]
                t = P.recip(f0, ps[6], waits=[tl1, t_prep] + wfin)
                ta = P.tt("dve", f0, ps[4], f0, ALU.mult, waits=[t])
                t = P.recip(f1, ps[7], waits=[ta])
                tb_ = P.tt("dve", f1, ps[5], f1, ALU.mult, waits=[t])
                for c in range(4):
                    bfree[c] = [tb_]
                ty = P.stt("dve", f2, f1, lams[:, 3:4], f0, ALU.mult, ALU.add, waits=[tb_])
                tsq = P.tt("dve", sqb, f2, f2, ALU.mult, waits=[ty])
                sk, sps, sfree = STr.next()
                t_ms = P.mm(sps, ones, sqb, True, True, waits=[tsq] + sfree, sig=True)
                lastpe[0] = t_ms
                t = P.ts("dve", f3, sps, 1.0 / 128.0, ALU.mult, EPS, ALU.add, waits=[t_ms])
                STr.release(sk, t)
                t = P.act(f3, f3, AF.Sqrt, waits=[t])
                t = P.recip(f3, f3, waits=[t])
                t = P.stt("dve", ybuf[:, j * 512:(j + 1) * 512], f2, gsc[:, 1:2], f3, ALU.mult, ALU.mult,
                          waits=[t] + (yfree if j == 0 else []))
                done.append(t)
            qr.release(qk, lastpe[0])
            kbr.release(kk, lastpe[0])
            t_o = P.dma("pool", yT[1024 + h * 128:1024 + (h + 1) * 128, :], ybuf, "st_y%d" % yk, waits=done)
            yr.release(yk, t_o)
            out_toks.append(t_o)
        P.emit(out_toks)
    return nc


def attn_masks(core, rel_bias_l):
    q = np.arange(512)
    k = np.arange(128)
    qc = (q // 64)[None, :]
    kh = (k >= 64).astype(np.int64)[:, None]
    negt = np.full((128, 512), NEG, np.float32)
    bmA = np.empty((6, 128, 512), np.float32)
    for i in range(5):
        dlt = qc - ((2 * i - 2) + kh)
        bmA[i] = np.where((dlt >= 0) & (dlt <= 2), np.float32(0.0), np.float32(NEG))
    bmA[5] = negt if core == 0 else bmA[0]
    bmB = np.empty((4, 128, 512), np.float32)
    for r in range(4):
        bmB[r] = np.where((2 * r + kh) <= qc, np.float32(0.0), np.float32(NEG))
    bmC = np.empty((4, 12, 128, 512), np.float32)
    for i in range(8):
        dlt = qc - ((2 * i - 8) + kh)
        valid = (dlt >= 0) & (dlt <= 8)
        dist = q[None, :] - k[:, None] - 128 * i + 512
        idx = np.clip(dist, -256, 256) + 256
        for h in range(4):
            bmC[h, i] = np.where(valid, rel_bias_l[h][idx], np.float32(NEG))
            if i < 4:
                bmC[h, 8 + i] = negt if core == 0 else bmC[h, i]
    others = [c for c in range(NCORES) if c != core]
    colB = np.empty((128, 56), np.float32)
    for t in range(56):
        colB[:, t] = 0.0 if others[t // 8] < core else NEG
    return bmA, bmB, bmC, colB


def run_attn(res_in, l, inputs):
    kT_full = np.concatenate([np.asarray(r["kT"]) for r in res_in], axis=1)
    v_full = np.concatenate([np.asarray(r["v"]) for r in res_in], axis=0)
    bdt = kT_full.dtype
    lam_init = 0.8 - 0.6 * math.exp(-0.3 * l)
    lamv = np.concatenate([inputs["lambda_q1"][l], inputs["lambda_k1"][l],
                           inputs["lambda_q2"][l], inputs["lambda_k2"][l]])[None, :].astype(np.float32)
    in_maps = []
    for c in range(NCORES):
        t0 = c * TPC

        def win_k(rows, halo):
            out = np.zeros((rows.stop - rows.start, halo + TPC), bdt)
            lo = max(t0 - halo, 0)
            out[:, halo - (t0 - lo):] = kT_full[rows, lo:t0 + TPC]
            return out

        def win_v(cols, halo):
            out = np.zeros((halo + TPC, cols.stop - cols.start), bdt)
            lo = max(t0 - halo, 0)
            out[halo - (t0 - lo):, :] = v_full[lo:t0 + TPC, cols]
            return out

        order = [c] + [o for o in range(NCORES) if o != c]
        kB = np.concatenate([kT_full[256:768, o * TPC:(o + 1) * TPC] for o in order], axis=1)
        vB = np.concatenate([v_full[o * TPC:(o + 1) * TPC, 256:768] for o in order], axis=0)
        bmA, bmB, bmC, colB = attn_masks(c, inputs["rel_bias"][l])
        in_maps.append({
            "qT": np.asarray(res_in[c]["qT"]),
            "kA": win_k(slice(0, 256), 128), "vA": win_v(slice(0, 256), 128),
            "kB": np.ascontiguousarray(kB), "vB": np.ascontiguousarray(vB),
            "kC": win_k(slice(768, 1280), 512), "vC": win_v(slice(768, 1280), 512),
            "bmA": bmA, "bmB": bmB, "bmC": bmC, "colB": colB,
            "sinks": np.ascontiguousarray(inputs["sinks"][l][None, :]),
            "lamv": lamv,
            "gsub": np.ascontiguousarray(inputs["diff_norm_g"][l][None, :]),
            "cst": np.array([[lam_init, 1.0 - lam_init]], np.float32),
        })
    return launch("attn", build_attn, in_maps)


def ln_pass(P, A, x_d, u_d, g_d, b_d, out_d, pre, u_toks):
    gt = A.alloc([D], F32)
    bt = A.alloc([D], F32)
    xts = [A.alloc([D], F32) for _ in range(2)]
    uts = [A.alloc([D], F32) for _ in range(2)]
    stt_ = A.alloc([8, 8], F32)
    t_g = P.dma("sp", gt, g_d.partition_broadcast(128), "ld_lng", waits=pre)
    t_g = P.dma("sp", bt, b_d.partition_broadcast(128), "ld_lng", waits=pre)
    xr = Ring(xts)
    ur = Ring(uts)
    outs = []
    for tt in range(8):
        rows = slice(tt * 128, (tt + 1) * 128)
        xk, xb, xfree = xr.next()
        t_x = P.dma("sp", xb, x_d[rows, :], "ld_lx%d" % xk, waits=xfree + pre)
        uk, ub, ufree = ur.next()
        t_u = P.dma("sp", ub, u_d[rows, :], "ld_lu%d" % uk, waits=ufree + pre + u_toks)
        s = lambda a: stt_[:, tt, a:a + 1]
        t = P.stt("dve", ub, xb, ALPHA, ub, ALU.mult, ALU.add, waits=[t_x, t_u])
        t = P.rsum("dve", s(0), ub, waits=[t])
        t = P.ts("dve", s(1), s(0), -1.0 / D, ALU.mult, waits=[t])
        t_sq = P.act(xb, ub, AF.Square, bias=s(1), waits=[t])
        t = P.rsum("dve", s(2), xb, waits=[t_sq])
        t = P.ts("dve", s(3), s(2), 1.0 / D, ALU.mult, EPS, ALU.add, waits=[t])
        t = P.act(s(3), s(3), AF.Sqrt, waits=[t])
        t = P.recip(s(4), s(3), waits=[t])
        t = P.ts("dve", ub, ub, s(1), ALU.add, s(4), ALU.mult, waits=[t])
        t = P.tt("dve", ub, ub, gt, ALU.mult, waits=[t, t_g])
        t = P.tt("dve", xb, ub, bt, ALU.add, waits=[t])
        ur.release(uk, t)
        t_o = P.dma("pool", out_d[rows, :], xb, "st_ln%d" % xk, waits=[t])
        xr.release(xk, t_o)
        outs.append(t_o)
    return outs


def load_w_piece(P, slot_ap, w_view, nk, sem, waits):
    t = None
    step = 8
    for g in range(0, nk, step):
        e = min(nk, g + step)
        t = P.dma("pool", slot_ap[:, g:e, :], w_view[:, g:e, :], sem, waits=waits)
    return t


def build_mix():
    nc = new_nc()
    din = lambda n, s, d: nc.dram_tensor(n, s, d, kind="ExternalInput").ap()
    x_d = din("x", [TPC, D], F32)
    xT_d = din("xT", [D, TPC], F32)
    yT_d = din("yT", [D, TPC], BF16)
    wg_d = din("w_gate", [D, 3 * D], F32)
    bg_d = din("b_gate", [128, 48], F32)
    wbr_d = din("w_br", [D, D], F32)
    wo_d = din("w_out", [D, D], F32)
    g_d = din("ln_g", [1, D], F32)
    b_d = din("ln_b", [1, D], F32)
    x1_d = nc.dram_tensor("x1", [TPC, D], F32, kind="ExternalOutput").ap()
    u_d = nc.dram_tensor("u_scr", [TPC, D], F32).ap()
    P = Prog(nc)
    with ExitStack() as st:
        st.enter_context(nc.allow_low_precision("bf16 matmul operands, fp32 accumulation"))
        arena_t = st.enter_context(nc.sbuf_tensor("arena", [128, 102000], BF16))
        A = Arena(arena_t)
        ps = [st.enter_context(nc.psum_tensor("ps%d" % i, [128, 512], F32))[:, :] for i in range(8)]
        mixT = A.alloc([16, TPC], BF16)
        bg = A.alloc([48], F32)
        sig = [A.alloc([512], F32) for _ in range(3)]
        accb = [A.alloc([512], F32) for _ in range(2)]
        tmpb = [A.alloc([512], F32) for _ in range(2)]
        stg = [A.alloc([512], F32) for _ in range(3)]
        mark = A.mark()
        wsl = [A.alloc([16, 512], BF16) for _ in range(5)]
        xTb = A.alloc([16, TPC], BF16)
        yTb = A.alloc([16, TPC], BF16)

        t_bg = P.dma("sp", bg, bg_d, "ld_bg")
        xTv = xT_d.rearrange("(k p) t -> p k t", p=128)
        for g in range(4):
            t_x = P.dma("pool", xTb[:, g * 4:(g + 1) * 4, :], xTv[:, g * 4:(g + 1) * 4, :], "ld_x")
        yTv = yT_d.rearrange("(k p) t -> p k t", p=128)
        for g in range(4):
            t_y = P.dma("sp", yTb[:, g * 4:(g + 1) * 4, :], yTv[:, g * 4:(g + 1) * 4, :], "ld_y")

        wr = Ring(wsl)
        pr = Ring(ps)
        sr = Ring(sig)
        ar = Ring(accb)
        tr = Ring(tmpb)
        gr = Ring(stg)

        pieces = []
        for nb in range(4):
            for g in range(3):
                pieces.append(("g", nb, g, wg_d[:, g * D + nb * 512: g * D + (nb + 1) * 512]))
            pieces.append(("br", nb, 0, wbr_d[:, nb * 512:(nb + 1) * 512]))
        for nb in range(4):
            pieces.append(("o", nb, 0, wo_d[:, nb * 512:(nb + 1) * 512]))
        loaded = {}
        nxt = [0]

        def prefetch(upto):
            while nxt[0] < len(pieces) and nxt[0] <= upto:
                kind, nb, g, wv = pieces[nxt[0]]
                k, slot, fr = wr.next()
                t = load_w_piece(P, slot, wv.rearrange("(k p) n -> p k n", p=128), 16, "ld_w%d" % k, fr)
                loaded[nxt[0]] = (k, slot, t)
                nxt[0] += 1

        lastpe = [None]
        for nb in range(4):
            base = nb * 4
            prefetch(base + 4)
            pg = [loaded[base + g] for g in range(3)]
            pb = loaded[base + 3]
            for c in range(4):
                chunk = nb * 4 + c
                for tb in range(2):
                    tsl = slice(tb * 512, (tb + 1) * 512)
                    ak, acc, afree = ar.next()
                    t_acc = None
                    for g in range(3):
                        gk, gps, gfree = pr.next()
                        for kc in range(16):
                            t_m = P.mm(gps, pg[g][1][:, kc, c * 128:(c + 1) * 128], xTb[:, kc, tsl], kc == 0, kc == 15,
                                       waits=[t_x, pg[g][2]] + gfree if kc == 0 else (), sig=(kc == 15))
                        sk, sg, sfree = sr.next()
                        t_s = P.act(sg, gps, AF.Sigmoid, bias=bg[:, g * 16 + chunk:g * 16 + chunk + 1],
                                    waits=[t_m, t_bg] + sfree)
                        pr.release(gk, t_s)
                        k0, k1 = ((0, 8), (8, 12), (12, 16))[g]
                        bk, bps, bfree = pr.next()
                        for kc in range(k0, k1):
                            t_m = P.mm(bps, pb[1][:, kc, c * 128:(c + 1) * 128], yTb[:, kc, tsl], kc == k0, kc == k1 - 1,
                                       waits=[t_y, pb[2]] + bfree if kc == k0 else (), sig=(kc == k1 - 1))
                        lastpe[0] = t_m
                        if g == 0:
                            t_acc = P.tt("dve", acc, bps, sg, ALU.mult, waits=[t_m, t_s] + afree)
                            pr.release(bk, t_acc)
                            sr.release(sk, t_acc)
                        else:
                            tk, tmp, tfree = tr.next()
                            t_t = P.tt("dve", tmp, bps, sg, ALU.mult, waits=[t_m, t_s] + tfree)
                            pr.release(bk, t_t)
                            sr.release(sk, t_t)
                            dst = acc if g == 1 else mixT[:, chunk, tsl]
                            t_acc = P.tt("dve", dst, acc, tmp, ALU.add, waits=[t_t, t_acc])
                            tr.release(tk, t_acc)
                    ar.release(ak, t_acc)
            for g in range(3):
                wr.release(pg[g][0], lastpe[0])
            wr.release(pb[0], lastpe[0])
        t_mix = ("dve", P.semcnt["dve"])
        u_toks = []
        for nb in range(4):
            prefetch(16 + nb + 1)
            k, slot, t_w = loaded[16 + nb]
            for tt in range(8):
                pk, pps, pfree = pr.next()
                for kc in range(16):
                    t_m = P.mm(pps, mixT[:, kc, tt * 128:(tt + 1) * 128], slot[:, kc, :], kc == 0, kc == 15,
                               waits=[t_mix, t_w] + pfree if kc == 0 else (), sig=(kc == 15))
                lastpe[0] = t_m
                sk, sb, sfree = gr.next()
                t_c = P.copy("act", sb, pps, waits=[t_m] + sfree)
                pr.release(pk, t_c)
                t_o = P.dma("pool", u_d[tt * 128:(tt + 1) * 128, nb * 512:(nb + 1) * 512], sb, "st_u%d" % sk, waits=[t_c])
                gr.release(sk, t_o)
                u_toks.append(t_o)
            wr.release(k, lastpe[0])
        pre = [lastpe[0]]
        A.reset(mark)
        outs = ln_pass(P, A, x_d, u_d, g_d, b_d, x1_d, pre, u_toks)
        P.emit(outs)
    return nc


def build_ffn():
    nc = new_nc()
    din = lambda n, s, d: nc.dram_tensor(n, s, d, kind="ExternalInput").ap()
    x_d = din("x1", [TPC, D], F32)
    xT_d = din("x1T", [D, TPC], F32)
    wi_d = din("w_ffn_in", [D, 2 * FFN], F32)
    wo_d = din("w_ffn_out", [FFN, D], F32)
    g_d = din("ln_g", [1, D], F32)
    b_d = din("ln_b", [1, D], F32)
    x2_d = nc.dram_tensor("x2", [TPC, D], F32, kind="ExternalOutput").ap()
    o_d = nc.dram_tensor("o_scr", [TPC, D], F32).ap()
    P = Prog(nc)
    NH = FFN // 128
    with ExitStack() as st:
        st.enter_context(nc.allow_low_precision("bf16 matmul operands, fp32 accumulation"))
        arena_t = st.enter_context(nc.sbuf_tensor("arena", [128, 102000], BF16))
        A = Arena(arena_t)
        ps = [st.enter_context(nc.psum_tensor("ps%d" % i, [128, 512], F32))[:, :] for i in range(8)]
        gT = A.alloc([NH, TPC], BF16)
        mark = A.mark()
        xTb = A.alloc([16, TPC], BF16)
        wsl = [A.alloc([NH * 256], BF16) for _ in range(3)]
        sgb = [A.alloc([512], F32) for _ in range(3)]
        stg = [A.alloc([256], F32) for _ in range(3)]

        xTv = xT_d.rearrange("(k p) t -> p k t", p=128)
        for g in range(4):
            t_x = P.dma("pool", xTb[:, g * 4:(g + 1) * 4, :], xTv[:, g * 4:(g + 1) * 4, :], "ld_x")
        wr = Ring(wsl)
        pr = Ring(ps)
        sr = Ring(sgb)
        gr = Ring(stg)
        pieces = []
        for blk in range(11):
            pieces.append(("in", wi_d[:, blk * 512:(blk + 1) * 512]))
            pieces.append(("in", wi_d[:, FFN + blk * 512: FFN + (blk + 1) * 512]))
        for nb in range(8):
            pieces.append(("out", wo_d[:, nb * 256:(nb + 1) * 256]))
        loaded = {}
        nxt = [0]

        def prefetch(upto):
            while nxt[0] < len(pieces) and nxt[0] <= upto:
                kind, wv = pieces[nxt[0]]
                k, slot, fr = wr.next()
                if kind == "in":
                    v = slot[:, 0:16 * 512].rearrange("p (k n) -> p k n", k=16)
                    t = load_w_piece(P, v, wv.rearrange("(k p) n -> p k n", p=128), 16, "ld_w%d" % k, fr)
                else:
                    v = slot.rearrange("p (k n) -> p k n", k=NH)
                    t = load_w_piece(P, v, wv.rearrange("(k p) n -> p k n", p=128), NH, "ld_w%d" % k, fr)
                loaded[nxt[0]] = (k, v, t)
                nxt[0] += 1

        lastpe = [None]
        for blk in range(11):
            prefetch(2 * blk + 2)
            kg, wgv, t_wg = loaded[2 * blk]
            ku, wuv, t_wu = loaded[2 * blk + 1]
            for c in range(4):
                hc = blk * 4 + c
                for tb in range(2):
                    tsl = slice(tb * 512, (tb + 1) * 512)
                    gk, gps, gfree = pr.next()
                    for kc in range(16):
                        t_mg = P.mm(gps, wgv[:, kc, c * 128:(c + 1) * 128], xTb[:, kc, tsl], kc == 0, kc == 15,
                                    waits=[t_x, t_wg] + gfree if kc == 0 else (), sig=(kc == 15))
                    uk, ups, ufree = pr.next()
                    for kc in range(16):
                        t_mu = P.mm(ups, wuv[:, kc, c * 128:(c + 1) * 128], xTb[:, kc, tsl], kc == 0, kc == 15,
                                    waits=[t_x, t_wu] + ufree if kc == 0 else (), sig=(kc == 15))
                    lastpe[0] = t_mu
                    sk, sg, sfree = sr.next()
                    t_s = P.act(sg, gps, AF.Silu, waits=[t_mg] + sfree)
                    pr.release(gk, t_s)
                    t_h = P.tt("dve", gT[:, hc, tsl], ups, sg, ALU.mult, waits=[t_mu, t_s])
                    pr.release(uk, t_h)
                    sr.release(sk, t_h)
            wr.release(kg, lastpe[0])
            wr.release(ku, lastpe[0])
        t_hid = ("dve", P.semcnt["dve"])
        o_toks = []
        for nb in range(8):
            prefetch(22 + nb + 1)
            k, wv, t_w = loaded[22 + nb]
            for tt in range(8):
                pk, pps, pfree = pr.next()
                for kc in range(NH):
                    t_m = P.mm(pps[:, 0:256], gT[:, kc, tt * 128:(tt + 1) * 128], wv[:, kc, :], kc == 0, kc == NH - 1,
                               waits=[t_hid, t_w] + pfree if kc == 0 else (), sig=(kc == NH - 1))
                lastpe[0] = t_m
                sk, sb, sfree = gr.next()
                t_c = P.copy("act", sb, pps[:, 0:256], waits=[t_m] + sfree)
                pr.release(pk, t_c)
                t_o = P.dma("pool", o_d[tt * 128:(tt + 1) * 128, nb * 256:(nb + 1) * 256], sb, "st_o%d" % sk, waits=[t_c])
                gr.release(sk, t_o)
                o_toks.append(t_o)
            wr.release(k, lastpe[0])
        pre = [lastpe[0], ("act", P.semcnt["act"]), ("dve", P.semcnt["dve"])]
        A.reset(mark)
        outs = ln_pass(P, A, x_d, o_d, g_d, b_d, x2_d, pre, o_toks)
        P.emit(outs)
    return nc


def run_mix(x_list, xT_list, yT_list, l, inputs):
    wbr = np.ascontiguousarray(np.concatenate([inputs["w_br_a"][l], inputs["w_br_b"][l], inputs["w_br_c"][l]], axis=0))
    bg = np.ascontiguousarray(inputs["b_gate"][l].reshape(48, 128).T)
    common = {"w_gate": np.ascontiguousarray(inputs["w_gate"][l]), "b_gate": bg, "w_br": wbr,
              "w_out": np.ascontiguousarray(inputs["w_out"][l]),
              "ln_g": np.ascontiguousarray(inputs["ln1_g"][l][None, :]),
              "ln_b": np.ascontiguousarray(inputs["ln1_b"][l][None, :])}
    in_maps = [dict(common, x=x_list[c], xT=xT_list[c], yT=yT_list[c]) for c in range(NCORES)]
    return launch("mix", build_mix, in_maps)


def run_ffn(x1_list, x1T_list, l, inputs):
    common = {"w_ffn_in": np.ascontiguousarray(inputs["w_ffn_in"][l]),
              "w_ffn_out": np.ascontiguousarray(inputs["w_ffn_out"][l]),
              "ln_g": np.ascontiguousarray(inputs["ln2_g"][l][None, :]),
              "ln_b": np.ascontiguousarray(inputs["ln2_b"][l][None, :])}
    in_maps = [dict(common, x1=x1_list[c], x1T=x1T_list[c]) for c in range(NCORES)]
    return launch("ffn", build_ffn, in_maps)


def kernel(**inputs):
    inputs = {k: np.asarray(v) for k, v in inputs.items()}
    x = np.ascontiguousarray(inputs["x"][0].astype(np.float32))
    xs = [np.ascontiguousarray(x[c * TPC:(c + 1) * TPC]) for c in range(NCORES)]
    for l in range(DEPTH):
        xTs = [np.ascontiguousarray(a.T) for a in xs]
        r_in = run_in(xTs, np.ascontiguousarray(inputs["w_in"][l]))
        r_at = run_attn(r_in, l, inputs)
        r_mx = run_mix(xs, xTs, [np.asarray(r["yT"]) for r in r_at], l, inputs)
        x1s = [np.asarray(r["x1"]) for r in r_mx]
        x1Ts = [np.ascontiguousarray(a.T) for a in x1s]
        r_ff = run_ffn(x1s, x1Ts, l, inputs)
        xs = [np.asarray(r["x2"]) for r in r_ff]
    out = np.concatenate(xs, axis=0)[None, :, :].astype(np.float32)
    return out
```
